# Optimizing a Trainium2 kernel written in Bass

```python
import jax, jax.numpy as jnp
from jax import lax
import numpy as np

D_MODEL = 1024
BATCH = 2
SEQ = 8192
DEPTH = 2

CHUNK = 64
EPS = 1e-6
CONV_K = 3
SGU_WIDTH = 512
SGU_GROUPS = 8
SGU_HEAD = SGU_WIDTH // SGU_GROUPS
SGU_BLOCK = 128
CONV_WIDTH = 512
POOL_WIDTH = 512
POOL_WINDOWS = (2, 4, 8, 16)
POOL_GROUPS = len(POOL_WINDOWS)
POOL_HEAD = POOL_WIDTH // POOL_GROUPS
N_BRANCH = 3
IN_COLS = 2 * SGU_WIDTH + 3 * CONV_WIDTH + POOL_WIDTH + N_BRANCH * D_MODEL
D_FF = 2816

kernel_name = "hybrid_gated_gmlp_shortconv_pool_block"


def _rmsnorm(x, g):
    x32 = x.astype(jnp.float32)
    y = x32 * lax.rsqrt(jnp.mean(x32 * x32, axis=-1, keepdims=True) + EPS)
    return (y * g.astype(jnp.float32)).astype(x.dtype)


def _causal_dwconv(x, w):
    k = w.shape[0]
    s = x.shape[1]
    xp = jnp.pad(x, ((0, 0), (k - 1, 0), (0, 0)))
    y = xp[:, 0:s, :] * w[0]
    for i in range(1, k):
        y = y + xp[:, i:i + s, :] * w[i]
    return y


def _sgu(u, v, ln_g, w_s, b_s):
    bn, s, _ = u.shape
    nb = s // SGU_BLOCK
    v = v.reshape(bn, nb, SGU_BLOCK, SGU_GROUPS, SGU_HEAD)
    v32 = v.astype(jnp.float32)
    mu = jnp.mean(v32, axis=-1, keepdims=True)
    var = jnp.mean(jnp.square(v32 - mu), axis=-1, keepdims=True)
    vn = ((v32 - mu) * lax.rsqrt(var + EPS)).astype(u.dtype) * ln_g.reshape(SGU_GROUPS, SGU_HEAD)
    chunk_id = jnp.arange(SGU_BLOCK) // CHUNK
    mask = chunk_id[None, :] <= chunk_id[:, None]
    w = jnp.where(mask[None], w_s, 0)
    sp = jnp.einsum('gij,bnjgc->bnigc', w, vn) + b_s.T[None, None, :, :, None]
    return u * sp.reshape(bn, s, SGU_WIDTH)


def _pool_mixer(z, w_pool, scale):
    bn, s, _ = z.shape
    zg = z.reshape(bn, s, POOL_GROUPS, POOL_HEAD)
    cs = jnp.cumsum(zg.astype(jnp.float32), axis=1)
    t = jnp.arange(s, dtype=jnp.float32)
    outs = []
    for g, win in enumerate(POOL_WINDOWS):
        c = cs[:, :, g]
        prev = jnp.pad(c[:, :s - win], ((0, 0), (win, 0), (0, 0)))
        count = jnp.minimum(t + 1.0, float(win))[None, :, None]
        outs.append((c - prev) / count - zg[:, :, g].astype(jnp.float32))
    pooled = jnp.stack(outs, axis=2).astype(z.dtype)
    y = jnp.einsum('bsgc,gcd->bsgd', pooled, w_pool)
    return y.reshape(bn, s, POOL_WIDTH) * scale


def setup_inputs(seed: int = 0) -> dict:
    key = jax.random.key(seed)
    ks = jax.random.split(key, 20)
    f = jnp.float32
    n = lambda k, shp, sc: jax.random.normal(k, shp, f) * sc
    return {
        "x": n(ks[0], (BATCH, SEQ, D_MODEL), 1.0),
        "norm_mix": 1.0 + n(ks[1], (DEPTH, D_MODEL), 0.02),
        "w_in": n(ks[2], (DEPTH, D_MODEL, IN_COLS), D_MODEL ** -0.5),
        "sgu_ln": 1.0 + n(ks[3], (DEPTH, SGU_WIDTH), 0.02),
        "sgu_w": n(ks[4], (DEPTH, SGU_GROUPS, SGU_BLOCK, SGU_BLOCK), 0.5 * SGU_BLOCK ** -0.5),
        "sgu_b": 1.0 + n(ks[5], (DEPTH, SGU_GROUPS, SGU_BLOCK), 0.02),
        "conv_b_w": n(ks[6], (DEPTH, CONV_K, CONV_WIDTH), CONV_K ** -0.5),
        "pool_w": n(ks[7], (DEPTH, POOL_GROUPS, POOL_HEAD, POOL_HEAD), POOL_HEAD ** -0.5),
        "pool_scale": 1.0 + n(ks[8], (DEPTH, POOL_WIDTH), 0.02),
        "w_br_a": n(ks[9], (DEPTH, SGU_WIDTH, D_MODEL), SGU_WIDTH ** -0.5),
        "w_br_b": n(ks[10], (DEPTH, CONV_WIDTH, D_MODEL), CONV_WIDTH ** -0.5),
        "w_br_c": n(ks[11], (DEPTH, POOL_WIDTH, D_MODEL), POOL_WIDTH ** -0.5),
        "w_o": n(ks[12], (DEPTH, D_MODEL, D_MODEL), D_MODEL ** -0.5),
        "norm_ffn": 1.0 + n(ks[13], (DEPTH, D_MODEL), 0.02),
        "w_up": n(ks[14], (DEPTH, D_MODEL, 2 * D_FF), D_MODEL ** -0.5),
        "ffn_conv_w": n(ks[15], (DEPTH, CONV_K, 2 * D_FF), CONV_K ** -0.5),
        "w_down": n(ks[16], (DEPTH, D_FF, D_MODEL), D_FF ** -0.5),
        "norm_f": 1.0 + n(ks[17], (D_MODEL,), 0.02),
    }


def reference(x, norm_mix, w_in, sgu_ln, sgu_w, sgu_b, conv_b_w, pool_w, pool_scale,
              w_br_a, w_br_b, w_br_c, w_o, norm_ffn, w_up, ffn_conv_w, w_down, norm_f):
    bn, s, _ = x.shape
    splits = np.cumsum([SGU_WIDTH, SGU_WIDTH, CONV_WIDTH, CONV_WIDTH, CONV_WIDTH, POOL_WIDTH]).tolist()
    for l in range(DEPTH):
        h = _rmsnorm(x, norm_mix[l])
        p = h @ w_in[l]
        u, v, bg, cg, xb, zc, gl = jnp.split(p, splits, axis=-1)
        ya = _sgu(jax.nn.gelu(u, approximate=False), jax.nn.gelu(v, approximate=False),
                  sgu_ln[l], sgu_w[l], sgu_b[l])
        yb = bg * _causal_dwconv(cg * xb, conv_b_w[l])
        yc = _pool_mixer(zc, pool_w[l], pool_scale[l])
        g = jax.nn.sigmoid(gl).reshape(bn, s, N_BRANCH, D_MODEL)
        merged = (g[:, :, 0] * (ya @ w_br_a[l])
                  + g[:, :, 1] * (yb @ w_br_b[l])
                  + g[:, :, 2] * (yc @ w_br_c[l]))
        x = x + merged @ w_o[l]
        h = _rmsnorm(x, norm_ffn[l])
        a = _causal_dwconv(h @ w_up[l], ffn_conv_w[l])
        ag, av = jnp.split(a, 2, axis=-1)
        x = x + (jax.nn.silu(ag) * av) @ w_down[l]
    return _rmsnorm(x, norm_f)
```

```python
import numpy as np
from contextlib import ExitStack
import concourse.bass as bass
import concourse.mybir as mybir
from concourse.bass_utils import run_bass_kernel_spmd

F32 = mybir.dt.float32
BF16 = mybir.dt.bfloat16
AF = mybir.ActivationFunctionType
ALU = mybir.AluOpType
AX = mybir.AxisListType

D = 1024
L = 2
S = 8192
NCORES = 8
OWN = 2048
HALO = 256
TOK = OWN + HALO
NT = 768
NTILE = TOK // NT
ST = 384
NST = NT // ST
NBLK = NT // 128
DFF = 2816
NJ = DFF // 128
EPS = 1e-6
SLOT = 4608
NSLOT = 5
NSCR = 12
SCRW = 784

C_GMIX = 0
C_GFFN = 16
C_GFIN = 32
C_PSC = 40
C_CONVB = 48
C_FCONV = 72
C_INVC = 336
C_BST = 400
C_LNGP = 1424
C_EPS = 1432
NCST = 1436
B_WST = 0
B_POOLW = 2048
B_ONES = 3072
B_MASK = 3200
NCSTB = 3328

YA, YB, YC, PL, MG = 0, 4, 8, 12, 16


class Buf:
    __slots__ = ("w", "r")

    def __init__(self):
        self.w = None
        self.r = {}


class Trk:
    def __init__(self, nc, es):
        self.nc = nc
        self.eng = {'pe': nc.tensor, 'act': nc.scalar, 'dve': nc.vector, 'pool': nc.gpsimd, 'sp': nc.sync}
        self.sems = {}
        self.cnt = {}
        self.es = es
        self.waited = {}
        for k in self.eng:
            self.newsem(k)

    def newsem(self, k):
        self.sems[k] = self.es.enter_context(self.nc.semaphore(k))
        self.cnt[k] = 0

    def _deps(self, reads, writes):
        deps = {}
        for b in reads:
            if b.w is not None and deps.get(b.w[0], 0) < b.w[1]:
                deps[b.w[0]] = b.w[1]
        for b in writes:
            if b.w is not None and deps.get(b.w[0], 0) < b.w[1]:
                deps[b.w[0]] = b.w[1]
            for k, v in b.r.items():
                if deps.get(k, 0) < v:
                    deps[k] = v
        return deps

    def _wait(self, e, deps):
        for k, v in deps.items():
            if k == e and v > self.cnt[e]:
                continue
            if self.waited.get((e, k), 0) >= v:
                continue
            self.eng[e].wait_ge(self.sems[k], v)
            self.waited[(e, k)] = v

    def _record(self, me, reads, writes):
        for b in reads:
            if b.r.get(me[0], 0) < me[1]:
                b.r[me[0]] = me[1]
        for b in writes:
            b.w = me
            b.r = {}

    def op(self, e, fn, reads=(), writes=(), inc=True):
        self._wait(e, self._deps(reads, writes))
        ins = fn(self.eng[e])
        if inc:
            self.cnt[e] += 1
            ins.then_inc(self.sems[e], 1)
            me = (e, self.cnt[e])
        else:
            me = (e, self.cnt[e] + 1)
        self._record(me, reads, writes)
        return ins

    def dma(self, e, dsem, out, in_, reads=(), writes=()):
        self._wait(e, self._deps(reads, writes))
        ins = self.eng[e].dma_start(out=out, in_=in_)
        self.cnt[dsem] += 16
        ins.then_inc(self.sems[dsem], 16)
        self._record((dsem, self.cnt[dsem]), reads, writes)
        return ins


def build(layers, final_norm, ntiles=NTILE):
    nc = bass.Bass("TRN2", target_bir_lowering=False)

    def dr(name, shape, kind="ExternalInput"):
        return nc.dram_tensor(name, shape, F32, kind=kind).ap()

    xT = dr("xT", [D, TOK])
    w_in = dr("w_in", [L, D, 6144])
    w_br = [dr("w_br_a", [L, 512, D]), dr("w_br_b", [L, 512, D]), dr("w_br_c", [L, 512, D])]
    w_o = dr("w_o", [L, D, D])
    w_up = dr("w_up", [L, D, 2 * DFF])
    w_down = dr("w_down", [L, DFF, D])
    cst_d = dr("cst", [128, NCST])
    cstb_d = dr("cstb", [128, NCSTB])
    outT = dr("outT", [D, OWN], kind="ExternalOutput")

    kp = lambda ap: ap.rearrange("(k p) c -> p k c", p=128)
    xTv = kp(xT)
    outTv = kp(outT)
    w_in_v = [kp(w_in[l]) for l in range(L)]
    w_br_v = [[kp(w_br[i][l]) for l in range(L)] for i in range(3)]
    w_o_v = [kp(w_o[l]) for l in range(L)]
    w_up_v = [kp(w_up[l]) for l in range(L)]
    w_down_v = [kp(w_down[l]) for l in range(L)]

    with ExitStack() as es:
        T = Trk(nc, es)
        for k in ['dcst', 'dcstb', 'dx0', 'dx1', 'do0', 'do1'] + ['dw%d' % i for i in range(NSLOT)]:
            T.newsem(k)
        sb = lambda name, shape, dt: es.enter_context(nc.sbuf_tensor(name, shape, dt))
        xt = [sb("xt%d" % i, [128, 8, NT], F32) for i in range(2)]
        xB = [[[Buf() for _ in range(NST)] for _ in range(8)] for _ in range(2)]
        ht = sb("ht", [128, 8, NT], BF16)
        hB = [[Buf() for _ in range(NST)] for _ in range(8)]
        big = sb("big", [128, 24, NT], BF16)
        bigB = [[Buf() for _ in range(NST)] for _ in range(24)]
        scr = [sb("scr%d" % i, [128, SCRW], F32) for i in range(NSCR)]
        scrB = [Buf() for _ in range(NSCR)]
        sbf = [sb("sbf%d" % i, [128, 512], BF16) for i in range(4)]
        sbfB = [Buf() for _ in range(4)]
        stt = [sb("stat%d" % i, [128, 64], F32) for i in range(3)]
        sttB = [Buf() for _ in range(3)]
        slots = [sb("slot%d" % i, [128, SLOT], BF16) for i in range(NSLOT)]
        slotB = [Buf() for _ in range(NSLOT)]
        cst = sb("cst_sb", [128, NCST], F32)
        cstB = Buf()
        cstb = sb("cstb_sb", [128, NCSTB], BF16)
        cstbB = Buf()
        tailB_t = sb("tailB", [128, L * 4 * 2], F32)
        tailP_t = sb("tailP", [128, L * 4 * 15], F32)
        tailF_t = sb("tailF", [128, L * NJ * 2 * 2], F32)
        tailBB = [[Buf() for _ in range(4)] for _ in range(L)]
        tailPB = [[Buf() for _ in range(4)] for _ in range(L)]
        tailFB = [[Buf() for _ in range(NJ)] for _ in range(L)]
        banks = [es.enter_context(nc.psum_tensor("bank%d" % i, [128, 512], F32)) for i in range(8)]
        bankB = [Buf() for _ in range(8)]
        NROT = 6
        nbank = [banks[6], banks[7]]
        nbankB = [bankB[6], bankB[7]]
        vst = sb("vstat", [128, 6, 48], F32)
        vstB = Buf()
        es.enter_context(nc.Block())

        rr = {'scr': 0, 'sbf': 0, 'stat': 0, 'slot': 0, 'bank': 0}

        def nxt(kind, n):
            i = rr[kind]
            rr[kind] = (i + 1) % n
            return i

        scr_free = list(range(NSCR))
        scr_idx = {}

        def get_scr():
            assert scr_free, "scratch pool exhausted"
            i = scr_free.pop(0)
            scr_idx[id(scrB[i])] = i
            return scr[i], scrB[i]

        def rel(*bufs):
            for b in bufs:
                i = scr_idx[id(b)]
                assert i not in scr_free
                scr_free.append(i)

        def get_sbf():
            i = nxt('sbf', 4)
            return sbf[i], sbfB[i]

        def get_stat():
            i = nxt('stat', 3)
            return stt[i], sttB[i]

        def get_bank():
            i = nxt('bank', NROT)
            return banks[i], bankB[i]

        def load_group(pieces):
            i = nxt('slot', NSLOT)
            key = 'dw%d' % i
            T._wait('pool', T._deps((), [slotB[i]]))
            for ap, off in pieces:
                K, n = ap.shape[1], ap.shape[2]
                ins = nc.gpsimd.dma_start(out=slots[i][:, off:off + K * n].rearrange("p (k c) -> p k c", k=K), in_=ap)
                T.cnt[key] += 16
                ins.then_inc(T.sems[key], 16)
            T._record((key, T.cnt[key]), (), [slotB[i]])
            return slots[i], slotB[i]

        def sv(slot, off, K, n):
            return slot[:, off:off + K * n].rearrange("p (k c) -> p k c", k=K)

        def mm(bank_ap, bB, lhsT, rhs, reads, start, stop, inc_all=False):
            deps = T._deps(reads, [bB] if start else ())
            T._wait('pe', deps)
            ins = T.eng['pe'].matmul(bank_ap, lhsT=lhsT, rhs=rhs, start=start, stop=stop)
            if stop or inc_all:
                T.cnt['pe'] += 1
                ins.then_inc(T.sems['pe'], 1)
                me = ('pe', T.cnt['pe'])
            else:
                me = ('pe', T.cnt['pe'] + 1)
            T._record(me, reads, ())
            if start:
                bB.r = {}
            bB.w = me

        def sl(st):
            return slice(st * ST, (st + 1) * ST)

        cc = lambda c0, n=1: cst[:, c0:c0 + n]

        T.dma('sp', 'dcst', cst[:], cst_d[:, :], writes=[cstB])
        T.dma('pool', 'dcstb', cstb[:], cstb_d[:, :], writes=[cstbB])
        T.op('dve', lambda e: e.tensor_tensor(
            out=cstb[:, B_WST:B_WST + L * 8 * 128].rearrange("p (g i) -> p g i", g=L * 8),
            in0=cstb[:, B_WST:B_WST + L * 8 * 128].rearrange("p (g i) -> p g i", g=L * 8),
            in1=cstb[:, B_MASK:B_MASK + 128].unsqueeze(1).to_broadcast([128, L * 8, 128]),
            op=ALU.mult), reads=[cstbB], writes=[cstbB])
        allT = [b for row in tailBB for b in row]
        T.op('dve', lambda e: e.memset(tailB_t[:], 0.0), writes=allT)
        allT = [b for row in tailPB for b in row]
        T.op('dve', lambda e: e.memset(tailP_t[:], 0.0), writes=allT)
        allT = [b for row in tailFB for b in row]
        T.op('dve', lambda e: e.memset(tailF_t[:], 0.0), writes=allT)

        for i in range(8):
            T.op('dve', lambda e: e.memset(banks[i][:], 0.0), writes=[bankB[i]])
        ones = cstb[:, B_ONES:B_ONES + 128]
        eps_ap = cc(C_EPS)

        npend = []

        def norm_accum(xb, d, st, delay=2):
            npend.append((xb, d, st))
            while len(npend) > delay:
                _norm_accum(*npend.pop(0))

        def norm_flush():
            while npend:
                _norm_accum(*npend.pop(0))

        def _norm_accum(xb, d, st):
            sq, sqB = get_sbf()
            T.op('act', lambda e: e.activation(out=sq[:, 0:ST], in_=xt[xb][:, d, sl(st)], func=AF.Square),
                 reads=[xB[xb][d][st]], writes=[sqB])
            mm(nbank[st][:, 0:ST], nbankB[st], ones, sq[:, 0:ST], [sqB, cstbB], d == 0, d == 7, inc_all=True)

        def norm_finish_st(xb, gcol, to_h, st):
            norm_flush()
            bk, bB = nbank[st], nbankB[st]
            rs, rsB = get_scr()
            T.op('act', lambda e: e.activation(out=rs[:, 0:ST], in_=bk[:, 0:ST], func=AF.Sqrt, bias=eps_ap, scale=1.0),
                 reads=[bB, cstB], writes=[rsB])
            T.op('dve', lambda e: e.reciprocal(out=rs[:, 0:ST], in_=rs[:, 0:ST]), reads=[rsB], writes=[rsB])
            for k in range(8):
                if to_h:
                    T.op('dve', lambda e: e.scalar_tensor_tensor(
                        out=ht[:, k, sl(st)], in0=xt[xb][:, k, sl(st)], scalar=cc(gcol + k), in1=rs[:, 0:ST],
                        op0=ALU.mult, op1=ALU.mult), reads=[xB[xb][k][st], cstB, rsB], writes=[hB[k][st]])
                else:
                    T.op('dve', lambda e: e.scalar_tensor_tensor(
                        out=xt[xb][:, k, sl(st)], in0=xt[xb][:, k, sl(st)], scalar=cc(gcol + k), in1=rs[:, 0:ST],
                        op0=ALU.mult, op1=ALU.mult), reads=[xB[xb][k][st], cstB, rsB], writes=[xB[xb][k][st]])
            rel(rsB)

        LO = {'mix': [0, 0], 'up': [0, 0], 'down': [0, 0], 'u': [0, 0], 'b': [0, 0], 'p': [0, 0], 'g': [0, 0]}
        PH = ['mix']

        def proj_fm(slot, sB, off, K, ncols, col0, st, rhs_fn, rhs_bufs_fn, k_range=None, bank=None, ph=None):
            lo = LO[ph if ph is not None else PH[0]][st]
            bk, bB = bank if bank is not None else get_bank()
            v = sv(slot, off, K, ncols)
            ks = list(range(K)) if k_range is None else list(k_range)
            for k in ks:
                mm(bk[:, lo:ST], bB, v[:, k, col0:col0 + 128], rhs_fn(k, st)[:, lo:ST], [sB] + rhs_bufs_fn(k, st),
                   k == 0, k == K - 1)
            return bk, bB

        h_rhs = lambda k, st: ht[:, k, sl(st)]
        h_bufs = lambda k, st: [hB[k][st]]

        def mixer(l, ti, xb):
            PH[0] = 'mix'
            uslot, usB = load_group([(w_in_v[l][:, :, 0:512], 0)])
            vslot, vsB = load_group([(w_in_v[l][:, :, 512:1024], 0)])
            wv = sv(vslot, 0, 8, 512)
            u_units = [(c, st) for st in range(NST) for c in range(4)]

            def u_unit(i):
                c, st = u_units[i]
                bk, bB = proj_fm(uslot, usB, 0, 8, 512, c * 128, st, h_rhs, h_bufs, ph='u')
                T.op('act', lambda e: e.activation(out=big[:, YA + c, sl(st)], in_=bk[:, 0:ST], func=AF.Gelu),
                     reads=[bB], writes=[bigB[YA + c][st]])

            vgs = []
            for b in range(NBLK):
                st = b // (NBLK // NST)
                t0 = b * 128
                bk, bB = get_bank()
                if (b + 1) * 128 > LO['mix'][0] or st > 0:
                    for k in range(8):
                        mm(bk[:, 0:512], bB, ht[:, k, t0:t0 + 128], wv[:, k, :], [vsB, hB[k][st]], k == 0, k == 7)
                vg, vgB = get_scr()
                vgs.append((vg, vgB))
                T.op('act', lambda e: e.activation(out=vg[:, 0:512], in_=bk[:, 0:512], func=AF.Gelu),
                     reads=[bB], writes=[vgB])
                sq, sqB = get_scr()
                T.op('act', lambda e: e.activation(out=sq[:, 0:512], in_=vg[:, 0:512], func=AF.Square),
                     reads=[vgB], writes=[sqB])
                T.op('dve', lambda e: e.tensor_reduce(out=vst[:, 0, b * 8:(b + 1) * 8],
                                                      in_=vg[:, 0:512].rearrange("p (g c) -> p g c", g=8),
                                                      axis=AX.X, op=ALU.add), reads=[vgB], writes=[vstB])
                T.op('dve', lambda e: e.tensor_reduce(out=vst[:, 1, b * 8:(b + 1) * 8],
                                                      in_=sq[:, 0:512].rearrange("p (g c) -> p g c", g=8),
                                                      axis=AX.X, op=ALU.add), reads=[sqB], writes=[vstB])
                rel(sqB)
                u_unit(b)
            pslot, psB = load_group([(w_in_v[l][:, :, 2560:3072], 0)])
            W = NT + 15

            def p_stage_a(c):
                z, zB = get_scr()
                tp = tailP_t[:, (l * 4 + c) * 15:(l * 4 + c + 1) * 15]
                T.op('act', lambda e: e.copy(out=z[:, 0:15], in_=tp), reads=[tailPB[l][c]], writes=[zB])
                for st in range(NST):
                    bk, bB = proj_fm(pslot, psB, 0, 8, 512, c * 128, st, h_rhs, h_bufs, ph='p')
                    T.op('act', lambda e: e.copy(out=z[:, 15 + st * ST:15 + (st + 1) * ST], in_=bk[:, 0:ST]),
                         reads=[bB], writes=[zB])
                if ti + 1 < ntiles:
                    T.op('act', lambda e: e.copy(out=tp, in_=z[:, NT:NT + 15]), reads=[zB], writes=[tailPB[l][c]])
                return (c, z, zB)

            def p_stage_b(ctx):
                c, z, zB = ctx
                cur, curB = z, zB
                for s_ in range(c + 1):
                    sh = 2 ** s_
                    lo = 2 ** (s_ + 1) - 1
                    nx, nxB = get_scr()
                    T.op('dve', lambda e: e.tensor_tensor(out=nx[:, lo:W], in0=cur[:, lo:W], in1=cur[:, lo - sh:W - sh],
                                                          op=ALU.add), reads=[curB], writes=[nxB])
                    if curB is not zB:
                        rel(curB)
                    cur, curB = nx, nxB
                win = 2 ** (c + 1)
                plB = [bigB[PL + c][st] for st in range(NST)]
                T.op('dve', lambda e: e.scalar_tensor_tensor(out=big[:, PL + c, :], in0=cur[:, 15:W], scalar=1.0 / win,
                                                             in1=z[:, 15:W], op0=ALU.mult, op1=ALU.subtract),
                     reads=[curB, zB], writes=plB)
                if ti == 0:
                    fx, fxB = get_stat()
                    T.op('dve', lambda e: e.tensor_tensor(out=fx[:, 0:16], in0=cur[:, 15 + HALO:15 + HALO + 16],
                                                          in1=cc(C_INVC + c * 16, 16), op=ALU.mult),
                         reads=[curB, cstB], writes=[fxB])
                    T.op('dve', lambda e: e.tensor_tensor(out=big[:, PL + c, HALO:HALO + 16], in0=fx[:, 0:16],
                                                          in1=z[:, 15 + HALO:15 + HALO + 16], op=ALU.subtract),
                         reads=[fxB, zB], writes=[bigB[PL + c][0]])
                rel(curB, zB)

            def p_stage_c(c):
                pw = cstb[:, B_POOLW + (l * 4 + c) * 128:B_POOLW + (l * 4 + c + 1) * 128]
                for st in range(NST):
                    bk, bB = get_bank()
                    mm(bk[:, 0:ST], bB, pw, big[:, PL + c, sl(st)], [cstbB, bigB[PL + c][st]], True, True)
                    T.op('act', lambda e: e.activation(out=big[:, YC + c, sl(st)], in_=bk[:, 0:ST], func=AF.Copy,
                                                       scale=cc(C_PSC + l * 4 + c)),
                         reads=[bB, cstB], writes=[bigB[YC + c][st]])

            pctx = []
            pctx.append(p_stage_a(3))
            pctx.append(p_stage_a(2))
            T.op('act', lambda e: e.mul(out=vst[:, 2, :], in_=vst[:, 0, :], mul=1.0 / 64), reads=[vstB], writes=[vstB])
            T.op('dve', lambda e: e.tensor_tensor(out=vst[:, 3, :], in0=vst[:, 2, :], in1=vst[:, 2, :], op=ALU.mult),
                 reads=[vstB], writes=[vstB])
            T.op('dve', lambda e: e.scalar_tensor_tensor(out=vst[:, 3, :], in0=vst[:, 1, :], scalar=1.0 / 64,
                                                         in1=vst[:, 3, :], op0=ALU.mult, op1=ALU.subtract),
                 reads=[vstB], writes=[vstB])
            T.op('act', lambda e: e.activation(out=vst[:, 4, :], in_=vst[:, 3, :], func=AF.Sqrt, bias=eps_ap, scale=1.0),
                 reads=[vstB, cstB], writes=[vstB])
            T.op('dve', lambda e: e.reciprocal(out=vst[:, 4, :], in_=vst[:, 4, :]), reads=[vstB], writes=[vstB])
            vns = {}

            def ln_apply(b):
                vg, vgB = vgs[b]
                vn, vnB = get_sbf()
                vg3 = vg[:, 0:512].rearrange("p (g c) -> p g c", g=8)
                T.op('dve', lambda e: e.tensor_tensor(out=vg3, in0=vg3,
                                                      in1=vst[:, 2, b * 8:(b + 1) * 8].unsqueeze(2).to_broadcast([128, 8, 64]),
                                                      op=ALU.subtract), reads=[vgB, vstB], writes=[vgB])
                T.op('dve', lambda e: e.tensor_tensor(out=vn[:, 0:512].rearrange("p (g c) -> p g c", g=8), in0=vg3,
                                                      in1=vst[:, 4, b * 8:(b + 1) * 8].unsqueeze(2).to_broadcast([128, 8, 64]),
                                                      op=ALU.mult), reads=[vgB, vstB], writes=[vnB])
                rel(vgB)
                vns[b] = (vn, vnB)

            ln_apply(0)
            for b in range(NBLK):
                st = b // (NBLK // NST)
                t0 = b * 128
                vn, vnB = vns[b]
                bk2, b2B = get_bank()
                for g in range(8):
                    c, hf = g // 2, g % 2
                    wst = cstb[:, B_WST + (l * 8 + g) * 128:B_WST + (l * 8 + g + 1) * 128]
                    T.op('pe', lambda e: e.matmul(bk2[hf * 64:(hf + 1) * 64, c * 128:(c + 1) * 128],
                                                  lhsT=vn[:, g * 64:(g + 1) * 64], rhs=wst, start=True, stop=True),
                         reads=[vnB, cstbB], writes=[b2B], inc=(g == 7))
                if b + 1 < NBLK:
                    ln_apply(b + 1)
                tmp, tmpB = get_scr()
                for c in range(4):
                    T.op('act', lambda e: e.activation(out=tmp[:, c * 128:(c + 1) * 128], in_=bk2[:, c * 128:(c + 1) * 128],
                                                       func=AF.Copy, scale=cc(C_LNGP + l * 4 + c)),
                         reads=[b2B, cstB], writes=[tmpB])
                T.op('dve', lambda e: e.tensor_tensor(out=tmp[:, 0:512], in0=tmp[:, 0:512],
                                                      in1=cst[:, C_BST + l * 512:C_BST + (l + 1) * 512], op=ALU.add),
                     reads=[tmpB, cstB], writes=[tmpB])
                yaB = [bigB[YA + c][st] for c in range(4)]
                T.op('dve', lambda e: e.tensor_tensor(out=big[:, YA:YA + 4, t0:t0 + 128],
                                                      in0=tmp[:, 0:512].rearrange("p (c i) -> p c i", c=4),
                                                      in1=big[:, YA:YA + 4, t0:t0 + 128], op=ALU.mult),
                     reads=[tmpB] + yaB, writes=yaB)
                rel(tmpB)
                if NBLK + b < len(u_units):
                    u_unit(NBLK + b)
                if b < 2:
                    pctx.append(p_stage_a(1 - b))
            def b_stage_a(c):
                slot, sB = load_group([(w_in_v[l][:, :, 1024 + c * 128:1024 + (c + 1) * 128], 0),
                                       (w_in_v[l][:, :, 1536 + c * 128:1536 + (c + 1) * 128], 1024),
                                       (w_in_v[l][:, :, 2048 + c * 128:2048 + (c + 1) * 128], 2048)])
                cgs, cgsB = get_scr()
                cx, cxB = get_scr()
                o, oB = get_scr()
                tb = tailB_t[:, (l * 4 + c) * 2:(l * 4 + c) * 2 + 2]
                for st in range(NST):
                    bk, bB = proj_fm(slot, sB, 1024, 8, 128, 0, st, h_rhs, h_bufs, ph='b')
                    T.op('act', lambda e: e.copy(out=cgs[:, sl(st)], in_=bk[:, 0:ST]), reads=[bB], writes=[cgsB])
                T.op('act', lambda e: e.copy(out=cx[:, 0:2], in_=tb), reads=[tailBB[l][c]], writes=[cxB])
                for st in range(NST):
                    bk, bB = proj_fm(slot, sB, 2048, 8, 128, 0, st, h_rhs, h_bufs, ph='b')
                    T.op('dve', lambda e: e.tensor_tensor(out=cx[:, 2 + st * ST:2 + (st + 1) * ST], in0=cgs[:, sl(st)],
                                                          in1=bk[:, 0:ST], op=ALU.mult),
                         reads=[bB, cgsB], writes=[cxB])
                if ti + 1 < ntiles:
                    T.op('act', lambda e: e.copy(out=tb, in_=cx[:, NT:NT + 2]), reads=[cxB], writes=[tailBB[l][c]])
                rel(cgsB)
                return (c, slot, sB, cx, cxB, o, oB)

            def b_stage_b(ctx):
                c, slot, sB, cx, cxB, o, oB = ctx
                wc = C_CONVB + (l * 4 + c) * 3
                T.op('act', lambda e: e.activation(out=o[:, 0:NT], in_=cx[:, 2:NT + 2], func=AF.Copy, scale=cc(wc + 2)),
                     reads=[cxB, cstB], writes=[oB])
                T.op('dve', lambda e: e.scalar_tensor_tensor(out=o[:, 0:NT], in0=cx[:, 1:NT + 1], scalar=cc(wc + 1),
                                                             in1=o[:, 0:NT], op0=ALU.mult, op1=ALU.add),
                     reads=[cxB, cstB, oB], writes=[oB])
                T.op('dve', lambda e: e.scalar_tensor_tensor(out=o[:, 0:NT], in0=cx[:, 0:NT], scalar=cc(wc),
                                                             in1=o[:, 0:NT], op0=ALU.mult, op1=ALU.add),
                     reads=[cxB, cstB, oB], writes=[oB])
                for st in range(NST):
                    bk, bB = proj_fm(slot, sB, 0, 8, 128, 0, st, h_rhs, h_bufs, ph='b')
                    T.op('dve', lambda e: e.tensor_tensor(out=big[:, YB + c, sl(st)], in0=o[:, sl(st)], in1=bk[:, 0:ST],
                                                          op=ALU.mult), reads=[bB, oB], writes=[bigB[YB + c][st]])
                rel(cxB, oB)

            ctxs = [b_stage_a(0)]
            p_stage_b(pctx[0])
            for c in range(1, 4):
                ctxs.append(b_stage_a(c))
                b_stage_b(ctxs[c - 1])
                p_stage_b(pctx[c])
                p_stage_c(pctx[c - 1][0])
            b_stage_b(ctxs[3])
            p_stage_c(pctx[3][0])
            for d in range(8):
                pieces = []
                for i in range(3):
                    pieces.append((w_in_v[l][:, :, 3072 + i * 1024 + d * 128:3072 + i * 1024 + (d + 1) * 128], i * 1024))
                for i in range(3):
                    pieces.append((w_br_v[i][l][:, :, d * 128:(d + 1) * 128], 3072 + i * 512))
                slot, sB = load_group(pieces)
                for st in range(NST):
                    ts = []
                    g0 = LO['g'][st]

                    def g_gate(i):
                        bkg, bgB = proj_fm(slot, sB, i * 1024, 8, 128, 0, st, h_rhs, h_bufs, ph='g')
                        sg, sgB = get_scr()
                        T.op('act', lambda e: e.activation(out=sg[:, g0:ST], in_=bkg[:, g0:ST], func=AF.Sigmoid),
                             reads=[bgB], writes=[sgB])
                        ts.append((sg, sgB))

                    def g_branch(i):
                        sg, sgB = ts[i]
                        ybase = (YA, YB, YC)[i]
                        bkb, bbB = proj_fm(slot, sB, 3072 + i * 512, 4, 128, 0, st,
                                           lambda k, st_, yb=ybase: big[:, yb + k, sl(st_)],
                                           lambda k, st_, yb=ybase: [bigB[yb + k][st_]], ph='g')
                        T.op('dve', lambda e: e.tensor_tensor(out=sg[:, g0:ST], in0=sg[:, g0:ST], in1=bkb[:, g0:ST], op=ALU.mult),
                             reads=[sgB, bbB], writes=[sgB])

                    if d == 0 and st == 0:
                        for i in range(3):
                            g_gate(i)
                        for i in range(3):
                            g_branch(i)
                    else:
                        for i in range(3):
                            g_gate(i)
                            g_branch(i)
                    T.op('dve', lambda e: e.tensor_tensor(out=ts[0][0][:, g0:ST], in0=ts[0][0][:, g0:ST], in1=ts[1][0][:, g0:ST],
                                                          op=ALU.add), reads=[ts[0][1], ts[1][1]], writes=[ts[0][1]])
                    T.op('dve', lambda e: e.tensor_tensor(out=big[:, MG + d, st * ST + g0:(st + 1) * ST], in0=ts[0][0][:, g0:ST],
                                                          in1=ts[2][0][:, g0:ST], op=ALU.add),
                         reads=[ts[0][1], ts[2][1]], writes=[bigB[MG + d][st]])
                    rel(ts[0][1], ts[1][1], ts[2][1])
            m_rhs = lambda k, st_: big[:, MG + k, sl(st_)]
            m_bufs = lambda k, st_: [bigB[MG + k][st_]]
            for gi in range(2):
                slot, sB = load_group([(w_o_v[l][:, :, gi * 512:(gi + 1) * 512], 0)])
                for st in range(NST):
                    for dd in range(4):
                        d = gi * 4 + dd
                        bk, bB = proj_fm(slot, sB, 0, 8, 512, dd * 128, st, m_rhs, m_bufs, ph='g')
                        T.op('dve', lambda e: e.tensor_tensor(out=xt[xb][:, d, sl(st)], in0=xt[xb][:, d, sl(st)],
                                                              in1=bk[:, 0:ST], op=ALU.add),
                             reads=[bB, xB[xb][d][st]], writes=[xB[xb][d][st]])
                        norm_accum(xb, d, st)
                    if gi == 1:
                        norm_finish_st(xb, C_GFFN + l * 8, True, st)

        def ffn(l, ti, xb, next_norm):
            PH[0] = 'up'
            fin_pending = []

            lo_up = LO['up'][0]
            e0 = lo_up + 2 if lo_up > 0 else 0

            def ffn_fin(j, os_):
                (og, ogB), (ov, ovB) = os_
                T.op('act', lambda e: e.activation(out=og[:, e0:NT], in_=og[:, e0:NT], func=AF.Silu),
                     reads=[ogB], writes=[ogB])
                T.op('dve', lambda e: e.tensor_tensor(out=big[:, j, e0:NT], in0=og[:, e0:NT], in1=ov[:, e0:NT], op=ALU.mult),
                     reads=[ogB, ovB], writes=[bigB[j][0], bigB[j][1]])
                rel(ogB, ovB)

            for jp in range(NJ // 2):
                j0 = 2 * jp
                slot, sB = load_group([(w_up_v[l][:, :, j0 * 128:(j0 + 2) * 128], 0),
                                       (w_up_v[l][:, :, DFF + j0 * 128:DFF + (j0 + 2) * 128], 2048)])
                pre = {}
                if jp == 0:
                    for jj in range(2):
                        for part in range(2):
                            pre[(jj, part)] = proj_fm(slot, sB, part * 2048, 8, 256, jj * 128, 0, h_rhs, h_bufs)
                for jj in range(2):
                    j = j0 + jj
                    os_ = []
                    for part in range(2):
                        a, aB = get_scr()
                        o, oB = get_scr()
                        tf = tailF_t[:, ((l * NJ + j) * 2 + part) * 2:((l * NJ + j) * 2 + part) * 2 + 2]
                        if e0 == 0:
                            T.op('act', lambda e: e.copy(out=a[:, 0:2], in_=tf), reads=[tailFB[l][j]], writes=[aB])
                        for st in range(NST):
                            if jp == 0 and st == 0:
                                bk, bB = pre[(jj, part)]
                            else:
                                bk, bB = proj_fm(slot, sB, part * 2048, 8, 256, jj * 128, st, h_rhs, h_bufs)
                            c0 = LO['up'][st]
                            T.op('act', lambda e: e.copy(out=a[:, 2 + st * ST + c0:2 + (st + 1) * ST], in_=bk[:, c0:ST]),
                                 reads=[bB], writes=[aB])
                        if ti + 1 < ntiles:
                            T.op('act', lambda e: e.copy(out=tf, in_=a[:, NT:NT + 2]), reads=[aB], writes=[tailFB[l][j]])
                        wc = C_FCONV + (l * 2 * NJ + part * NJ + j) * 3
                        T.op('act', lambda e: e.activation(out=o[:, e0:NT], in_=a[:, e0 + 2:NT + 2], func=AF.Copy,
                                                           scale=cc(wc + 2)), reads=[aB, cstB], writes=[oB])
                        T.op('dve', lambda e: e.scalar_tensor_tensor(out=o[:, e0:NT], in0=a[:, e0 + 1:NT + 1], scalar=cc(wc + 1),
                                                                     in1=o[:, e0:NT], op0=ALU.mult, op1=ALU.add),
                             reads=[aB, cstB, oB], writes=[oB])
                        T.op('dve', lambda e: e.scalar_tensor_tensor(out=o[:, e0:NT], in0=a[:, e0:NT], scalar=cc(wc),
                                                                     in1=o[:, e0:NT], op0=ALU.mult, op1=ALU.add),
                             reads=[aB, cstB, oB], writes=[oB])
                        rel(aB)
                        os_.append((o, oB))
                    if fin_pending:
                        ffn_fin(*fin_pending.pop(0))
                    fin_pending.append((j, os_))
            while fin_pending:
                ffn_fin(*fin_pending.pop(0))
            a_rhs = lambda k, st_: big[:, k, sl(st_)]
            a_bufs = lambda k, st_: [bigB[k][st_]]
            PH[0] = 'down'

            def down_unit(slot, sB, d, st, bank=None, k_range=None, finish=True):
                bk, bB = proj_fm(slot, sB, 0, NJ, 128, 0, st, a_rhs, a_bufs, k_range=k_range, bank=bank)
                if not finish:
                    return bk, bB
                T.op('dve', lambda e: e.tensor_tensor(out=xt[xb][:, d, sl(st)], in0=xt[xb][:, d, sl(st)],
                                                      in1=bk[:, 0:ST], op=ALU.add),
                     reads=[bB, xB[xb][d][st]], writes=[xB[xb][d][st]])
                norm_accum(xb, d, st)

            for d in range(6):
                slot, sB = load_group([(w_down_v[l][:, :, d * 128:(d + 1) * 128], 0)])
                if d == 0:
                    part = [down_unit(slot, sB, d, st, k_range=range(0, NJ - 2), finish=False) for st in range(NST)]
                    for st in range(NST):
                        down_unit(slot, sB, d, st, bank=part[st], k_range=range(NJ - 2, NJ))
                    continue
                for st in range(NST):
                    down_unit(slot, sB, d, st)
            s6 = load_group([(w_down_v[l][:, :, 6 * 128:7 * 128], 0)])
            s7 = load_group([(w_down_v[l][:, :, 7 * 128:8 * 128], 0)])
            for st in range(NST):
                down_unit(s6[0], s6[1], 6, st)
                down_unit(s7[0], s7[1], 7, st)
                if next_norm is not None:
                    norm_finish_st(xb, next_norm[0], next_norm[1], st)

        def load_x(ti):
            xb = ti % 2
            allx = [xB[xb][k][st] for k in range(8) for st in range(NST)]
            T.dma('pool', 'dx%d' % xb, xt[xb][:], xTv[:, :, ti * NT:(ti + 1) * NT], writes=allx)

        load_x(0)
        for ti in range(ntiles):
            xb = ti % 2
            allx = [xB[xb][k][st] for k in range(8) for st in range(NST)]
            for st in range(NST):
                for d in range(8):
                    norm_accum(xb, d, st)
                norm_finish_st(xb, C_GMIX + layers[0] * 8, True, st)
            for li, l in enumerate(layers):
                last = (li == len(layers) - 1)
                if ti == 0:
                    LO['mix'][0] = 128 if last else 96
                    LO['up'][0] = 252 if last else 96
                    LO['down'][0] = 256 if last else 96
                    LO['u'][0] = LO['g'][0] = 252 if last else 96
                    LO['b'][0] = 248 if last else 96
                    LO['p'][0] = 236 if last else 96
                else:
                    for kk in LO:
                        LO[kk][0] = 0
                mixer(l, ti, xb)
                if li == len(layers) - 1 and ti + 1 < ntiles:
                    load_x(ti + 1)
                if li + 1 < len(layers):
                    nn = (C_GMIX + layers[li + 1] * 8, True)
                elif final_norm:
                    nn = (C_GFIN, False)
                else:
                    nn = None
                ffn(l, ti, xb, nn)
            norm_flush()
            lo = HALO if ti == 0 else 0
            o0 = ti * NT + lo - HALO
            T.dma('sp', 'do%d' % xb, outTv[:, :, o0:o0 + NT - lo], xt[xb][:, :, lo:NT], reads=allx)
        for xb in range(2):
            if T.cnt['do%d' % xb] > 0:
                nc.sync.wait_ge(T.sems['do%d' % xb], T.cnt['do%d' % xb])
    return nc


def _consts(inp, q):
    f = np.float32
    cst = np.zeros((128, NCST), f)
    pk = lambda v: np.ascontiguousarray(v.reshape(L, -1, 128).transpose(2, 0, 1))
    cst[:, C_GMIX:C_GMIX + 16] = pk(inp["norm_mix"]).reshape(128, 16)
    cst[:, C_GFFN:C_GFFN + 16] = pk(inp["norm_ffn"]).reshape(128, 16)
    cst[:, C_GFIN:C_GFIN + 8] = inp["norm_f"].reshape(8, 128).T
    cst[:, C_PSC:C_PSC + 8] = pk(inp["pool_scale"]).reshape(128, 8)
    cb = inp["conv_b_w"].reshape(L, 3, 4, 128).transpose(3, 0, 2, 1)
    cst[:, C_CONVB:C_CONVB + 24] = cb.reshape(128, 24)
    fc = inp["ffn_conv_w"].reshape(L, 3, 2 * NJ, 128).transpose(3, 0, 2, 1)
    cst[:, C_FCONV:C_FCONV + 264] = fc.reshape(128, 264)
    t = np.arange(16, dtype=f)
    for c in range(4):
        w = float(2 ** (c + 1))
        cnt = np.minimum(t + 1.0, w) if q == 0 else np.full(16, w, f)
        cst[:, C_INVC + c * 16:C_INVC + (c + 1) * 16] = (f(1.0) / cnt.astype(f))[None, :]
    bs = inp["sgu_b"].reshape(L, 4, 2, 128)
    bs = np.repeat(bs, 64, axis=2).transpose(2, 0, 1, 3)
    cst[:, C_BST:C_BST + 1024] = bs.reshape(128, 1024)
    cst[:, C_LNGP:C_LNGP + 8] = pk(inp["sgu_ln"]).reshape(128, 8)
    cst[:, C_EPS] = EPS
    return cst


def _constsb(inp):
    f = np.float32
    cb = np.zeros((128, NCSTB), f)
    wst = inp["sgu_w"].transpose(3, 0, 1, 2)
    cb[:, B_WST:B_WST + 2048] = wst.reshape(128, 2048)
    pw = inp["pool_w"].transpose(2, 0, 1, 3)
    cb[:, B_POOLW:B_POOLW + 1024] = pw.reshape(128, 1024)
    cb[:, B_ONES:B_ONES + 128] = 1.0 / 1024
    j = np.arange(128)[:, None] // 64
    i = np.arange(128)[None, :] // 64
    cb[:, B_MASK:B_MASK + 128] = (j <= i).astype(f)
    return cb


_NC_CACHE = {}


def _get_nc(layers, final_norm):
    key = (tuple(layers), final_norm)
    if key not in _NC_CACHE:
        _NC_CACHE[key] = build(list(layers), final_norm)
    return _NC_CACHE[key]


def _launch(x, inp, layers, final_norm):
    nc = _get_nc(layers, final_norm)
    cb = _constsb(inp)
    in_maps = []
    for core in range(NCORES):
        b, q = core // 4, core % 4
        s0 = q * OWN
        xs = np.zeros((TOK, D), np.float32)
        if q == 0:
            xs[HALO:] = x[b, 0:OWN]
        else:
            xs = x[b, s0 - HALO:s0 + OWN]
        in_maps.append({
            "xT": np.ascontiguousarray(xs.T),
            "w_in": inp["w_in"], "w_br_a": inp["w_br_a"], "w_br_b": inp["w_br_b"], "w_br_c": inp["w_br_c"],
            "w_o": inp["w_o"], "w_up": inp["w_up"], "w_down": inp["w_down"],
            "cst": _consts(inp, q), "cstb": cb,
        })
    res = run_bass_kernel_spmd(nc, in_maps, core_ids=list(range(NCORES)))
    out = np.empty((2, S, D), np.float32)
    for core in range(NCORES):
        b, q = core // 4, core % 4
        out[b, q * OWN:(q + 1) * OWN, :] = np.asarray(res.results[core]["outT"]).T
    return out


FUSED = True


def kernel(**inputs):
    inp = {k: np.ascontiguousarray(np.asarray(v, dtype=np.float32)) for k, v in inputs.items()}
    x = inp["x"]
    if FUSED:
        return _launch(x, inp, (0, 1), True)
    x = _launch(x, inp, (0,), False)
    return _launch(x, inp, (1,), True)
```

```python
import numpy as np
from contextlib import ExitStack
import concourse.bass as bass
import concourse.mybir as mybir
from concourse.bass_utils import run_bass_kernel_spmd

F32 = mybir.dt.float32
BF16 = mybir.dt.bfloat16
AF = mybir.ActivationFunctionType
ALU = mybir.AluOpType
AX = mybir.AxisListType

D = 1024
L = 2
S = 8192
NCORES = 8
OWN = 2048
HALO = 256
TOK = OWN + HALO
NT = 768
NTILE = TOK // NT
ST = 384
NST = NT // ST
NBLK = NT // 128
DFF = 2816
NJ = DFF // 128
EPS = 1e-6
SLOT = 4608
NSLOT = 5
NSCR = 12
SCRW = 784

C_GMIX = 0
C_GFFN = 16
C_GFIN = 32
C_PSC = 40
C_CONVB = 48
C_FCONV = 72
C_INVC = 336
C_BST = 400
C_LNGP = 1424
C_EPS = 1432
NCST = 1436
B_WST = 0
B_POOLW = 2048
B_ONES = 3072
B_MASK = 3200
NCSTB = 3328

YA, YB, YC, PL, MG = 0, 4, 8, 12, 16


class Buf:
    __slots__ = ("w", "r")

    def __init__(self):
        self.w = None
        self.r = {}


class Trk:
    def __init__(self, nc, es):
        self.nc = nc
        self.eng = {'pe': nc.tensor, 'act': nc.scalar, 'dve': nc.vector, 'pool': nc.gpsimd, 'sp': nc.sync}
        self.sems = {}
        self.cnt = {}
        self.es = es
        self.waited = {}
        for k in self.eng:
            self.newsem(k)

    def newsem(self, k):
        self.sems[k] = self.es.enter_context(self.nc.semaphore(k))
        self.cnt[k] = 0

    def _deps(self, reads, writes):
        deps = {}
        for b in reads:
            if b.w is not None and deps.get(b.w[0], 0) < b.w[1]:
                deps[b.w[0]] = b.w[1]
        for b in writes:
            if b.w is not None and deps.get(b.w[0], 0) < b.w[1]:
                deps[b.w[0]] = b.w[1]
            for k, v in b.r.items():
                if deps.get(k, 0) < v:
                    deps[k] = v
        return deps

    def _wait(self, e, deps):
        for k, v in deps.items():
            if k == e and v > self.cnt[e]:
                continue
            if self.waited.get((e, k), 0) >= v:
                continue
            self.eng[e].wait_ge(self.sems[k], v)
            self.waited[(e, k)] = v

    def _record(self, me, reads, writes):
        for b in reads:
            if b.r.get(me[0], 0) < me[1]:
                b.r[me[0]] = me[1]
        for b in writes:
            b.w = me
            b.r = {}

    def op(self, e, fn, reads=(), writes=(), inc=True):
        self._wait(e, self._deps(reads, writes))
        ins = fn(self.eng[e])
        if inc:
            self.cnt[e] += 1
            ins.then_inc(self.sems[e], 1)
            me = (e, self.cnt[e])
        else:
            me = (e, self.cnt[e] + 1)
        self._record(me, reads, writes)
        return ins

    def dma(self, e, dsem, out, in_, reads=(), writes=()):
        self._wait(e, self._deps(reads, writes))
        ins = self.eng[e].dma_start(out=out, in_=in_)
        self.cnt[dsem] += 16
        ins.then_inc(self.sems[dsem], 16)
        self._record((dsem, self.cnt[dsem]), reads, writes)
        return ins


def build(layers, final_norm, ntiles=NTILE):
    nc = bass.Bass("TRN2", target_bir_lowering=False)

    def dr(name, shape, kind="ExternalInput"):
        return nc.dram_tensor(name, shape, F32, kind=kind).ap()

    xT = dr("xT", [D, TOK])
    w_in = dr("w_in", [L, D, 6144])
    w_br = [dr("w_br_a", [L, 512, D]), dr("w_br_b", [L, 512, D]), dr("w_br_c", [L, 512, D])]
    w_o = dr("w_o", [L, D, D])
    w_up = dr("w_up", [L, D, 2 * DFF])
    w_down = dr("w_down", [L, DFF, D])
    cst_d = dr("cst", [128, NCST])
    cstb_d = dr("cstb", [128, NCSTB])
    outT = dr("outT", [D, OWN], kind="ExternalOutput")

    kp = lambda ap: ap.rearrange("(k p) c -> p k c", p=128)
    xTv = kp(xT)
    outTv = kp(outT)
    w_in_v = [kp(w_in[l]) for l in range(L)]
    w_br_v = [[kp(w_br[i][l]) for l in range(L)] for i in range(3)]
    w_o_v = [kp(w_o[l]) for l in range(L)]
    w_up_v = [kp(w_up[l]) for l in range(L)]
    w_down_v = [kp(w_down[l]) for l in range(L)]

    with ExitStack() as es:
        T = Trk(nc, es)
        for k in ['dcst', 'dcstb', 'dx0', 'dx1', 'do0', 'do1'] + ['dw%d' % i for i in range(NSLOT)]:
            T.newsem(k)
        sb = lambda name, shape, dt: es.enter_context(nc.sbuf_tensor(name, shape, dt))
        xt = [sb("xt%d" % i, [128, 8, NT], F32) for i in range(2)]
        xB = [[[Buf() for _ in range(NST)] for _ in range(8)] for _ in range(2)]
        ht = sb("ht", [128, 8, NT], BF16)
        hB = [[Buf() for _ in range(NST)] for _ in range(8)]
        big = sb("big", [128, 24, NT], BF16)
        bigB = [[Buf() for _ in range(NST)] for _ in range(24)]
        scr = [sb("scr%d" % i, [128, SCRW], F32) for i in range(NSCR)]
        scrB = [Buf() for _ in range(NSCR)]
        sbf = [sb("sbf%d" % i, [128, 512], BF16) for i in range(4)]
        sbfB = [Buf() for _ in range(4)]
        stt = [sb("stat%d" % i, [128, 64], F32) for i in range(3)]
        sttB = [Buf() for _ in range(3)]
        slots = [sb("slot%d" % i, [128, SLOT], BF16) for i in range(NSLOT)]
        slotB = [Buf() for _ in range(NSLOT)]
        cst = sb("cst_sb", [128, NCST], F32)
        cstB = Buf()
        cstb = sb("cstb_sb", [128, NCSTB], BF16)
        cstbB = Buf()
        tailB_t = sb("tailB", [128, L * 4 * 2], F32)
        tailP_t = sb("tailP", [128, L * 4 * 15], F32)
        tailF_t = sb("tailF", [128, L * NJ * 2 * 2], F32)
        tailBB = [[Buf() for _ in range(4)] for _ in range(L)]
        tailPB = [[Buf() for _ in range(4)] for _ in range(L)]
        tailFB = [[Buf() for _ in range(NJ)] for _ in range(L)]
        banks = [es.enter_context(nc.psum_tensor("bank%d" % i, [128, 512], F32)) for i in range(8)]
        bankB = [Buf() for _ in range(8)]
        NROT = 6
        nbank = [banks[6], banks[7]]
        nbankB = [bankB[6], bankB[7]]
        vst = sb("vstat", [128, 6, 48], F32)
        vstB = Buf()
        es.enter_context(nc.Block())

        rr = {'scr': 0, 'sbf': 0, 'stat': 0, 'slot': 0, 'bank': 0}

        def nxt(kind, n):
            i = rr[kind]
            rr[kind] = (i + 1) % n
            return i

        scr_free = list(range(NSCR))
        scr_idx = {}

        def get_scr():
            assert scr_free, "scratch pool exhausted"
            i = scr_free.pop(0)
            scr_idx[id(scrB[i])] = i
            return scr[i], scrB[i]

        def rel(*bufs):
            for b in bufs:
                i = scr_idx[id(b)]
                assert i not in scr_free
                scr_free.append(i)

        def get_sbf():
            i = nxt('sbf', 4)
            return sbf[i], sbfB[i]

        def get_stat():
            i = nxt('stat', 3)
            return stt[i], sttB[i]

        def get_bank():
            i = nxt('bank', NROT)
            return banks[i], bankB[i]

        def load_group(pieces):
            i = nxt('slot', NSLOT)
            key = 'dw%d' % i
            T._wait('pool', T._deps((), [slotB[i]]))
            for ap, off in pieces:
                K, n = ap.shape[1], ap.shape[2]
                ins = nc.gpsimd.dma_start(out=slots[i][:, off:off + K * n].rearrange("p (k c) -> p k c", k=K), in_=ap)
                T.cnt[key] += 16
                ins.then_inc(T.sems[key], 16)
            T._record((key, T.cnt[key]), (), [slotB[i]])
            return slots[i], slotB[i]

        def sv(slot, off, K, n):
            return slot[:, off:off + K * n].rearrange("p (k c) -> p k c", k=K)

        def mm(bank_ap, bB, lhsT, rhs, reads, start, stop, inc_all=False):
            deps = T._deps(reads, [bB] if start else ())
            T._wait('pe', deps)
            ins = T.eng['pe'].matmul(bank_ap, lhsT=lhsT, rhs=rhs, start=start, stop=stop)
            if stop or inc_all:
                T.cnt['pe'] += 1
                ins.then_inc(T.sems['pe'], 1)
                me = ('pe', T.cnt['pe'])
            else:
                me = ('pe', T.cnt['pe'] + 1)
            T._record(me, reads, ())
            if start:
                bB.r = {}
            bB.w = me

        def sl(st):
            return slice(st * ST, (st + 1) * ST)

        cc = lambda c0, n=1: cst[:, c0:c0 + n]

        T.dma('sp', 'dcst', cst[:], cst_d[:, :], writes=[cstB])
        T.dma('pool', 'dcstb', cstb[:], cstb_d[:, :], writes=[cstbB])
        T.op('dve', lambda e: e.tensor_tensor(
            out=cstb[:, B_WST:B_WST + L * 8 * 128].rearrange("p (g i) -> p g i", g=L * 8),
            in0=cstb[:, B_WST:B_WST + L * 8 * 128].rearrange("p (g i) -> p g i", g=L * 8),
            in1=cstb[:, B_MASK:B_MASK + 128].unsqueeze(1).to_broadcast([128, L * 8, 128]),
            op=ALU.mult), reads=[cstbB], writes=[cstbB])
        allT = [b for row in tailBB for b in row]
        T.op('dve', lambda e: e.memset(tailB_t[:], 0.0), writes=allT)
        allT = [b for row in tailPB for b in row]
        T.op('dve', lambda e: e.memset(tailP_t[:], 0.0), writes=allT)
        allT = [b for row in tailFB for b in row]
        T.op('dve', lambda e: e.memset(tailF_t[:], 0.0), writes=allT)

        for i in range(8):
            T.op('dve', lambda e: e.memset(banks[i][:], 0.0), writes=[bankB[i]])
        ones = cstb[:, B_ONES:B_ONES + 128]
        eps_ap = cc(C_EPS)

        npend = []

        def norm_accum(xb, d, st, delay=2):
            npend.append((xb, d, st))
            while len(npend) > delay:
                _norm_accum(*npend.pop(0))

        def norm_flush():
            while npend:
                _norm_accum(*npend.pop(0))

        def _norm_accum(xb, d, st):
            sq, sqB = get_sbf()
            T.op('act', lambda e: e.activation(out=sq[:, 0:ST], in_=xt[xb][:, d, sl(st)], func=AF.Square),
                 reads=[xB[xb][d][st]], writes=[sqB])
            mm(nbank[st][:, 0:ST], nbankB[st], ones, sq[:, 0:ST], [sqB, cstbB], d == 0, d == 7, inc_all=True)

        def norm_finish_st(xb, gcol, to_h, st):
            norm_flush()
            bk, bB = nbank[st], nbankB[st]
            rs, rsB = get_scr()
            T.op('act', lambda e: e.activation(out=rs[:, 0:ST], in_=bk[:, 0:ST], func=AF.Sqrt, bias=eps_ap, scale=1.0),
                 reads=[bB, cstB], writes=[rsB])
            T.op('dve', lambda e: e.reciprocal(out=rs[:, 0:ST], in_=rs[:, 0:ST]), reads=[rsB], writes=[rsB])
            for k in range(8):
                if to_h:
                    T.op('dve', lambda e: e.scalar_tensor_tensor(
                        out=ht[:, k, sl(st)], in0=xt[xb][:, k, sl(st)], scalar=cc(gcol + k), in1=rs[:, 0:ST],
                        op0=ALU.mult, op1=ALU.mult), reads=[xB[xb][k][st], cstB, rsB], writes=[hB[k][st]])
                else:
                    T.op('dve', lambda e: e.scalar_tensor_tensor(
                        out=xt[xb][:, k, sl(st)], in0=xt[xb][:, k, sl(st)], scalar=cc(gcol + k), in1=rs[:, 0:ST],
                        op0=ALU.mult, op1=ALU.mult), reads=[xB[xb][k][st], cstB, rsB], writes=[xB[xb][k][st]])
            rel(rsB)

        LO = {'mix': [0, 0], 'up': [0, 0], 'down': [0, 0], 'u': [0, 0], 'b': [0, 0], 'p': [0, 0], 'g': [0, 0]}
        PH = ['mix']

        def proj_fm(slot, sB, off, K, ncols, col0, st, rhs_fn, rhs_bufs_fn, k_range=None, bank=None, ph=None):
            lo = LO[ph if ph is not None else PH[0]][st]
            bk, bB = bank if bank is not None else get_bank()
            v = sv(slot, off, K, ncols)
            ks = list(range(K)) if k_range is None else list(k_range)
            for k in ks:
                mm(bk[:, lo:ST], bB, v[:, k, col0:col0 + 128], rhs_fn(k, st)[:, lo:ST], [sB] + rhs_bufs_fn(k, st),
                   k == 0, k == K - 1)
            return bk, bB

        h_rhs = lambda k, st: ht[:, k, sl(st)]
        h_bufs = lambda k, st: [hB[k][st]]

        def mixer(l, ti, xb):
            PH[0] = 'mix'
            uslot, usB = load_group([(w_in_v[l][:, :, 0:512], 0)])
            vslot, vsB = load_group([(w_in_v[l][:, :, 512:1024], 0)])
            wv = sv(vslot, 0, 8, 512)
            u_units = [(c, st) for st in range(NST) for c in range(4)]

            def u_unit(i):
                c, st = u_units[i]
                bk, bB = proj_fm(uslot, usB, 0, 8, 512, c * 128, st, h_rhs, h_bufs, ph='u')
                T.op('act', lambda e: e.activation(out=big[:, YA + c, sl(st)], in_=bk[:, 0:ST], func=AF.Gelu),
                     reads=[bB], writes=[bigB[YA + c][st]])

            vgs = []
            for b in range(NBLK):
                st = b // (NBLK // NST)
                t0 = b * 128
                bk, bB = get_bank()
                if (b + 1) * 128 > LO['mix'][0] or st > 0:
                    for k in range(8):
                        mm(bk[:, 0:512], bB, ht[:, k, t0:t0 + 128], wv[:, k, :], [vsB, hB[k][st]], k == 0, k == 7)
                vg, vgB = get_scr()
                vgs.append((vg, vgB))
                T.op('act', lambda e: e.activation(out=vg[:, 0:512], in_=bk[:, 0:512], func=AF.Gelu),
                     reads=[bB], writes=[vgB])
                sq, sqB = get_scr()
                T.op('act', lambda e: e.activation(out=sq[:, 0:512], in_=vg[:, 0:512], func=AF.Square),
                     reads=[vgB], writes=[sqB])
                T.op('dve', lambda e: e.tensor_reduce(out=vst[:, 0, b * 8:(b + 1) * 8],
                                                      in_=vg[:, 0:512].rearrange("p (g c) -> p g c", g=8),
                                                      axis=AX.X, op=ALU.add), reads=[vgB], writes=[vstB])
                T.op('dve', lambda e: e.tensor_reduce(out=vst[:, 1, b * 8:(b + 1) * 8],
                                                      in_=sq[:, 0:512].rearrange("p (g c) -> p g c", g=8),
                                                      axis=AX.X, op=ALU.add), reads=[sqB], writes=[vstB])
                rel(sqB)
                u_unit(b)
            pslot, psB = load_group([(w_in_v[l][:, :, 2560:3072], 0)])
            W = NT + 15

            def p_stage_a(c):
                z, zB = get_scr()
                tp = tailP_t[:, (l * 4 + c) * 15:(l * 4 + c + 1) * 15]
                T.op('act', lambda e: e.copy(out=z[:, 0:15], in_=tp), reads=[tailPB[l][c]], writes=[zB])
                for st in range(NST):
                    bk, bB = proj_fm(pslot, psB, 0, 8, 512, c * 128, st, h_rhs, h_bufs, ph='p')
                    T.op('act', lambda e: e.copy(out=z[:, 15 + st * ST:15 + (st + 1) * ST], in_=bk[:, 0:ST]),
                         reads=[bB], writes=[zB])
                if ti + 1 < ntiles:
                    T.op('act', lambda e: e.copy(out=tp, in_=z[:, NT:NT + 15]), reads=[zB], writes=[tailPB[l][c]])
                return (c, z, zB)

            def p_stage_b(ctx):
                c, z, zB = ctx
                cur, curB = z, zB
                for s_ in range(c + 1):
                    sh = 2 ** s_
                    lo = 2 ** (s_ + 1) - 1
                    nx, nxB = get_scr()
                    T.op('dve', lambda e: e.tensor_tensor(out=nx[:, lo:W], in0=cur[:, lo:W], in1=cur[:, lo - sh:W - sh],
                                                          op=ALU.add), reads=[curB], writes=[nxB])
                    if curB is not zB:
                        rel(curB)
                    cur, curB = nx, nxB
                win = 2 ** (c + 1)
                plB = [bigB[PL + c][st] for st in range(NST)]
                T.op('dve', lambda e: e.scalar_tensor_tensor(out=big[:, PL + c, :], in0=cur[:, 15:W], scalar=1.0 / win,
                                                             in1=z[:, 15:W], op0=ALU.mult, op1=ALU.subtract),
                     reads=[curB, zB], writes=plB)
                if ti == 0:
                    fx, fxB = get_stat()
                    T.op('dve', lambda e: e.tensor_tensor(out=fx[:, 0:16], in0=cur[:, 15 + HALO:15 + HALO + 16],
                                                          in1=cc(C_INVC + c * 16, 16), op=ALU.mult),
                         reads=[curB, cstB], writes=[fxB])
                    T.op('dve', lambda e: e.tensor_tensor(out=big[:, PL + c, HALO:HALO + 16], in0=fx[:, 0:16],
                                                          in1=z[:, 15 + HALO:15 + HALO + 16], op=ALU.subtract),
                         reads=[fxB, zB], writes=[bigB[PL + c][0]])
                rel(curB, zB)

            def p_stage_c(c):
                pw = cstb[:, B_POOLW + (l * 4 + c) * 128:B_POOLW + (l * 4 + c + 1) * 128]
                for st in range(NST):
                    bk, bB = get_bank()
                    mm(bk[:, 0:ST], bB, pw, big[:, PL + c, sl(st)], [cstbB, bigB[PL + c][st]], True, True)
                    T.op('act', lambda e: e.activation(out=big[:, YC + c, sl(st)], in_=bk[:, 0:ST], func=AF.Copy,
                                                       scale=cc(C_PSC + l * 4 + c)),
                         reads=[bB, cstB], writes=[bigB[YC + c][st]])

            pctx = []
            pctx.append(p_stage_a(3))
            pctx.append(p_stage_a(2))
            T.op('act', lambda e: e.mul(out=vst[:, 2, :], in_=vst[:, 0, :], mul=1.0 / 64), reads=[vstB], writes=[vstB])
            T.op('dve', lambda e: e.tensor_tensor(out=vst[:, 3, :], in0=vst[:, 2, :], in1=vst[:, 2, :], op=ALU.mult),
                 reads=[vstB], writes=[vstB])
            T.op('dve', lambda e: e.scalar_tensor_tensor(out=vst[:, 3, :], in0=vst[:, 1, :], scalar=1.0 / 64,
                                                         in1=vst[:, 3, :], op0=ALU.mult, op1=ALU.subtract),
                 reads=[vstB], writes=[vstB])
            T.op('act', lambda e: e.activation(out=vst[:, 4, :], in_=vst[:, 3, :], func=AF.Sqrt, bias=eps_ap, scale=1.0),
                 reads=[vstB, cstB], writes=[vstB])
            T.op('dve', lambda e: e.reciprocal(out=vst[:, 4, :], in_=vst[:, 4, :]), reads=[vstB], writes=[vstB])
            vns = {}

            def ln_apply(b):
                vg, vgB = vgs[b]
                vn, vnB = get_sbf()
                vg3 = vg[:, 0:512].rearrange("p (g c) -> p g c", g=8)
                T.op('dve', lambda e: e.tensor_tensor(out=vg3, in0=vg3,
                                                      in1=vst[:, 2, b * 8:(b + 1) * 8].unsqueeze(2).to_broadcast([128, 8, 64]),
                                                      op=ALU.subtract), reads=[vgB, vstB], writes=[vgB])
                T.op('dve', lambda e: e.tensor_tensor(out=vn[:, 0:512].rearrange("p (g c) -> p g c", g=8), in0=vg3,
                                                      in1=vst[:, 4, b * 8:(b + 1) * 8].unsqueeze(2).to_broadcast([128, 8, 64]),
                                                      op=ALU.mult), reads=[vgB, vstB], writes=[vnB])
                rel(vgB)
                vns[b] = (vn, vnB)

            ln_apply(0)
            for b in range(NBLK):
                st = b // (NBLK // NST)
                t0 = b * 128
                vn, vnB = vns[b]
                bk2, b2B = get_bank()
                for g in range(8):
                    c, hf = g // 2, g % 2
                    wst = cstb[:, B_WST + (l * 8 + g) * 128:B_WST + (l * 8 + g + 1) * 128]
                    T.op('pe', lambda e: e.matmul(bk2[hf * 64:(hf + 1) * 64, c * 128:(c + 1) * 128],
                                                  lhsT=vn[:, g * 64:(g + 1) * 64], rhs=wst, start=True, stop=True),
                         reads=[vnB, cstbB], writes=[b2B], inc=(g == 7))
                if b + 1 < NBLK:
                    ln_apply(b + 1)
                tmp, tmpB = get_scr()
                for c in range(4):
                    T.op('act', lambda e: e.activation(out=tmp[:, c * 128:(c + 1) * 128], in_=bk2[:, c * 128:(c + 1) * 128],
                                                       func=AF.Copy, scale=cc(C_LNGP + l * 4 + c)),
                         reads=[b2B, cstB], writes=[tmpB])
                T.op('dve', lambda e: e.tensor_tensor(out=tmp[:, 0:512], in0=tmp[:, 0:512],
                                                      in1=cst[:, C_BST + l * 512:C_BST + (l + 1) * 512], op=ALU.add),
                     reads=[tmpB, cstB], writes=[tmpB])
                yaB = [bigB[YA + c][st] for c in range(4)]
                T.op('dve', lambda e: e.tensor_tensor(out=big[:, YA:YA + 4, t0:t0 + 128],
                                                      in0=tmp[:, 0:512].rearrange("p (c i) -> p c i", c=4),
                                                      in1=big[:, YA:YA + 4, t0:t0 + 128], op=ALU.mult),
                     reads=[tmpB] + yaB, writes=yaB)
                rel(tmpB)
                if NBLK + b < len(u_units):
                    u_unit(NBLK + b)
                if b < 2:
                    pctx.append(p_stage_a(1 - b))
            def b_stage_a(c):
                slot, sB = load_group([(w_in_v[l][:, :, 1024 + c * 128:1024 + (c + 1) * 128], 0),
                                       (w_in_v[l][:, :, 1536 + c * 128:1536 + (c + 1) * 128], 1024),
                                       (w_in_v[l][:, :, 2048 + c * 128:2048 + (c + 1) * 128], 2048)])
                cgs, cgsB = get_scr()
                cx, cxB = get_scr()
                o, oB = get_scr()
                tb = tailB_t[:, (l * 4 + c) * 2:(l * 4 + c) * 2 + 2]
                for st in range(NST):
                    bk, bB = proj_fm(slot, sB, 1024, 8, 128, 0, st, h_rhs, h_bufs, ph='b')
                    T.op('act', lambda e: e.copy(out=cgs[:, sl(st)], in_=bk[:, 0:ST]), reads=[bB], writes=[cgsB])
                T.op('act', lambda e: e.copy(out=cx[:, 0:2], in_=tb), reads=[tailBB[l][c]], writes=[cxB])
                for st in range(NST):
                    bk, bB = proj_fm(slot, sB, 2048, 8, 128, 0, st, h_rhs, h_bufs, ph='b')
                    T.op('dve', lambda e: e.tensor_tensor(out=cx[:, 2 + st * ST:2 + (st + 1) * ST], in0=cgs[:, sl(st)],
                                                          in1=bk[:, 0:ST], op=ALU.mult),
                         reads=[bB, cgsB], writes=[cxB])
                if ti + 1 < ntiles:
                    T.op('act', lambda e: e.copy(out=tb, in_=cx[:, NT:NT + 2]), reads=[cxB], writes=[tailBB[l][c]])
                rel(cgsB)
                return (c, slot, sB, cx, cxB, o, oB)

            def b_stage_b(ctx):
                c, slot, sB, cx, cxB, o, oB = ctx
                wc = C_CONVB + (l * 4 + c) * 3
                T.op('act', lambda e: e.activation(out=o[:, 0:NT], in_=cx[:, 2:NT + 2], func=AF.Copy, scale=cc(wc + 2)),
                     reads=[cxB, cstB], writes=[oB])
                T.op('dve', lambda e: e.scalar_tensor_tensor(out=o[:, 0:NT], in0=cx[:, 1:NT + 1], scalar=cc(wc + 1),
                                                             in1=o[:, 0:NT], op0=ALU.mult, op1=ALU.add),
                     reads=[cxB, cstB, oB], writes=[oB])
                T.op('dve', lambda e: e.scalar_tensor_tensor(out=o[:, 0:NT], in0=cx[:, 0:NT], scalar=cc(wc),
                                                             in1=o[:, 0:NT], op0=ALU.mult, op1=ALU.add),
                     reads=[cxB, cstB, oB], writes=[oB])
                for st in range(NST):
                    bk, bB = proj_fm(slot, sB, 0, 8, 128, 0, st, h_rhs, h_bufs, ph='b')
                    T.op('dve', lambda e: e.tensor_tensor(out=big[:, YB + c, sl(st)], in0=o[:, sl(st)], in1=bk[:, 0:ST],
                                                          op=ALU.mult), reads=[bB, oB], writes=[bigB[YB + c][st]])
                rel(cxB, oB)

            ctxs = [b_stage_a(0)]
            p_stage_b(pctx[0])
            for c in range(1, 4):
                ctxs.append(b_stage_a(c))
                b_stage_b(ctxs[c - 1])
                p_stage_b(pctx[c])
                p_stage_c(pctx[c - 1][0])
            b_stage_b(ctxs[3])
            p_stage_c(pctx[3][0])
            for d in range(8):
                pieces = []
                for i in range(3):
                    pieces.append((w_in_v[l][:, :, 3072 + i * 1024 + d * 128:3072 + i * 1024 + (d + 1) * 128], i * 1024))
                for i in range(3):
                    pieces.append((w_br_v[i][l][:, :, d * 128:(d + 1) * 128], 3072 + i * 512))
                slot, sB = load_group(pieces)
                for st in range(NST):
                    ts = []
                    g0 = LO['g'][st]

                    def g_gate(i):
                        bkg, bgB = proj_fm(slot, sB, i * 1024, 8, 128, 0, st, h_rhs, h_bufs, ph='g')
                        sg, sgB = get_scr()
                        T.op('act', lambda e: e.activation(out=sg[:, g0:ST], in_=bkg[:, g0:ST], func=AF.Sigmoid),
                             reads=[bgB], writes=[sgB])
                        ts.append((sg, sgB))

                    def g_branch(i):
                        sg, sgB = ts[i]
                        ybase = (YA, YB, YC)[i]
                        bkb, bbB = proj_fm(slot, sB, 3072 + i * 512, 4, 128, 0, st,
                                           lambda k, st_, yb=ybase: big[:, yb + k, sl(st_)],
                                           lambda k, st_, yb=ybase: [bigB[yb + k][st_]], ph='g')
                        T.op('dve', lambda e: e.tensor_tensor(out=sg[:, g0:ST], in0=sg[:, g0:ST], in1=bkb[:, g0:ST], op=ALU.mult),
                             reads=[sgB, bbB], writes=[sgB])

                    if d == 0 and st == 0:
                        for i in range(3):
                            g_gate(i)
                        for i in range(3):
                            g_branch(i)
                    else:
                        for i in range(3):
                            g_gate(i)
                            g_branch(i)
                    T.op('dve', lambda e: e.tensor_tensor(out=ts[0][0][:, g0:ST], in0=ts[0][0][:, g0:ST], in1=ts[1][0][:, g0:ST],
                                                          op=ALU.add), reads=[ts[0][1], ts[1][1]], writes=[ts[0][1]])
                    T.op('dve', lambda e: e.tensor_tensor(out=big[:, MG + d, st * ST + g0:(st + 1) * ST], in0=ts[0][0][:, g0:ST],
                                                          in1=ts[2][0][:, g0:ST], op=ALU.add),
                         reads=[ts[0][1], ts[2][1]], writes=[bigB[MG + d][st]])
                    rel(ts[0][1], ts[1][1], ts[2][1])
            m_rhs = lambda k, st_: big[:, MG + k, sl(st_)]
            m_bufs = lambda k, st_: [bigB[MG + k][st_]]
            for gi in range(2):
                slot, sB = load_group([(w_o_v[l][:, :, gi * 512:(gi + 1) * 512], 0)])
                for st in range(NST):
                    for dd in range(4):
                        d = gi * 4 + dd
                        bk, bB = proj_fm(slot, sB, 0, 8, 512, dd * 128, st, m_rhs, m_bufs, ph='g')
                        T.op('dve', lambda e: e.tensor_tensor(out=xt[xb][:, d, sl(st)], in0=xt[xb][:, d, sl(st)],
                                                              in1=bk[:, 0:ST], op=ALU.add),
                             reads=[bB, xB[xb][d][st]], writes=[xB[xb][d][st]])
                        norm_accum(xb, d, st)
                    if gi == 1:
                        norm_finish_st(xb, C_GFFN + l * 8, True, st)

        def ffn(l, ti, xb, next_norm):
            PH[0] = 'up'
            fin_pending = []

            lo_up = LO['up'][0]
            e0 = lo_up + 2 if lo_up > 0 else 0

            def ffn_fin(j, os_):
                (og, ogB), (ov, ovB) = os_
                T.op('act', lambda e: e.activation(out=og[:, e0:NT], in_=og[:, e0:NT], func=AF.Silu),
                     reads=[ogB], writes=[ogB])
                T.op('dve', lambda e: e.tensor_tensor(out=big[:, j, e0:NT], in0=og[:, e0:NT], in1=ov[:, e0:NT], op=ALU.mult),
                     reads=[ogB, ovB], writes=[bigB[j][0], bigB[j][1]])
                rel(ogB, ovB)

            for jp in range(NJ // 2):
                j0 = 2 * jp
                slot, sB = load_group([(w_up_v[l][:, :, j0 * 128:(j0 + 2) * 128], 0),
                                       (w_up_v[l][:, :, DFF + j0 * 128:DFF + (j0 + 2) * 128], 2048)])
                pre = {}
                if jp == 0:
                    for jj in range(2):
                        for part in range(2):
                            pre[(jj, part)] = proj_fm(slot, sB, part * 2048, 8, 256, jj * 128, 0, h_rhs, h_bufs)
                for jj in range(2):
                    j = j0 + jj
                    os_ = []
                    for part in range(2):
                        a, aB = get_scr()
                        o, oB = get_scr()
                        tf = tailF_t[:, ((l * NJ + j) * 2 + part) * 2:((l * NJ + j) * 2 + part) * 2 + 2]
                        if e0 == 0:
                            T.op('act', lambda e: e.copy(out=a[:, 0:2], in_=tf), reads=[tailFB[l][j]], writes=[aB])
                        for st in range(NST):
                            if jp == 0 and st == 0:
                                bk, bB = pre[(jj, part)]
                            else:
                                bk, bB = proj_fm(slot, sB, part * 2048, 8, 256, jj * 128, st, h_rhs, h_bufs)
                            c0 = LO['up'][st]
                            T.op('act', lambda e: e.copy(out=a[:, 2 + st * ST + c0:2 + (st + 1) * ST], in_=bk[:, c0:ST]),
                                 reads=[bB], writes=[aB])
                        if ti + 1 < ntiles:
                            T.op('act', lambda e: e.copy(out=tf, in_=a[:, NT:NT + 2]), reads=[aB], writes=[tailFB[l][j]])
                        wc = C_FCONV + (l * 2 * NJ + part * NJ + j) * 3
                        T.op('act', lambda e: e.activation(out=o[:, e0:NT], in_=a[:, e0 + 2:NT + 2], func=AF.Copy,
                                                           scale=cc(wc + 2)), reads=[aB, cstB], writes=[oB])
                        T.op('dve', lambda e: e.scalar_tensor_tensor(out=o[:, e0:NT], in0=a[:, e0 + 1:NT + 1], scalar=cc(wc + 1),
                                                                     in1=o[:, e0:NT], op0=ALU.mult, op1=ALU.add),
                             reads=[aB, cstB, oB], writes=[oB])
                        T.op('dve', lambda e: e.scalar_tensor_tensor(out=o[:, e0:NT], in0=a[:, e0:NT], scalar=cc(wc),
                                                                     in1=o[:, e0:NT], op0=ALU.mult, op1=ALU.add),
                             reads=[aB, cstB, oB], writes=[oB])
                        rel(aB)
                        os_.append((o, oB))
                    if fin_pending:
                        ffn_fin(*fin_pending.pop(0))
                    fin_pending.append((j, os_))
            while fin_pending:
                ffn_fin(*fin_pending.pop(0))
            a_rhs = lambda k, st_: big[:, k, sl(st_)]
            a_bufs = lambda k, st_: [bigB[k][st_]]
            PH[0] = 'down'

            def down_unit(slot, sB, d, st, bank=None, k_range=None, finish=True):
                bk, bB = proj_fm(slot, sB, 0, NJ, 128, 0, st, a_rhs, a_bufs, k_range=k_range, bank=bank)
                if not finish:
                    return bk, bB
                T.op('dve', lambda e: e.tensor_tensor(out=xt[xb][:, d, sl(st)], in0=xt[xb][:, d, sl(st)],
                                                      in1=bk[:, 0:ST], op=ALU.add),
                     reads=[bB, xB[xb][d][st]], writes=[xB[xb][d][st]])
                norm_accum(xb, d, st)

            for d in range(6):
                slot, sB = load_group([(w_down_v[l][:, :, d * 128:(d + 1) * 128], 0)])
                if d == 0:
                    part = [down_unit(slot, sB, d, st, k_range=range(0, NJ - 2), finish=False) for st in range(NST)]
                    for st in range(NST):
                        down_unit(slot, sB, d, st, bank=part[st], k_range=range(NJ - 2, NJ))
                    continue
                for st in range(NST):
                    down_unit(slot, sB, d, st)
            s6 = load_group([(w_down_v[l][:, :, 6 * 128:7 * 128], 0)])
            s7 = load_group([(w_down_v[l][:, :, 7 * 128:8 * 128], 0)])
            for st in range(NST):
                down_unit(s6[0], s6[1], 6, st)
                down_unit(s7[0], s7[1], 7, st)
                if next_norm is not None:
                    norm_finish_st(xb, next_norm[0], next_norm[1], st)

        def load_x(ti):
            xb = ti % 2
            allx = [xB[xb][k][st] for k in range(8) for st in range(NST)]
            T.dma('pool', 'dx%d' % xb, xt[xb][:], xTv[:, :, ti * NT:(ti + 1) * NT], writes=allx)

        load_x(0)
        for ti in range(ntiles):
            xb = ti % 2
            allx = [xB[xb][k][st] for k in range(8) for st in range(NST)]
            for st in range(NST):
                for d in range(8):
                    norm_accum(xb, d, st)
                norm_finish_st(xb, C_GMIX + layers[0] * 8, True, st)
            for li, l in enumerate(layers):
                last = (li == len(layers) - 1)
                if ti == 0:
                    LO['mix'][0] = 128 if last else 96
                    LO['up'][0] = 252 if last else 108
                    LO['down'][0] = 256 if last else 112
                    LO['u'][0] = LO['g'][0] = 252 if last else 108
                    LO['b'][0] = 248 if last else 108
                    LO['p'][0] = 236 if last else 96
                else:
                    for kk in LO:
                        LO[kk][0] = 0
                mixer(l, ti, xb)
                if li == len(layers) - 1 and ti + 1 < ntiles:
                    load_x(ti + 1)
                if li + 1 < len(layers):
                    nn = (C_GMIX + layers[li + 1] * 8, True)
                elif final_norm:
                    nn = (C_GFIN, False)
                else:
                    nn = None
                ffn(l, ti, xb, nn)
            norm_flush()
            lo = HALO if ti == 0 else 0
            o0 = ti * NT + lo - HALO
            T.dma('sp', 'do%d' % xb, outTv[:, :, o0:o0 + NT - lo], xt[xb][:, :, lo:NT], reads=allx)
        for xb in range(2):
            if T.cnt['do%d' % xb] > 0:
                nc.sync.wait_ge(T.sems['do%d' % xb], T.cnt['do%d' % xb])
    return nc


def _consts(inp, q):
    f = np.float32
    cst = np.zeros((128, NCST), f)
    pk = lambda v: np.ascontiguousarray(v.reshape(L, -1, 128).transpose(2, 0, 1))
    cst[:, C_GMIX:C_GMIX + 16] = pk(inp["norm_mix"]).reshape(128, 16)
    cst[:, C_GFFN:C_GFFN + 16] = pk(inp["norm_ffn"]).reshape(128, 16)
    cst[:, C_GFIN:C_GFIN + 8] = inp["norm_f"].reshape(8, 128).T
    cst[:, C_PSC:C_PSC + 8] = pk(inp["pool_scale"]).reshape(128, 8)
    cb = inp["conv_b_w"].reshape(L, 3, 4, 128).transpose(3, 0, 2, 1)
    cst[:, C_CONVB:C_CONVB + 24] = cb.reshape(128, 24)
    fc = inp["ffn_conv_w"].reshape(L, 3, 2 * NJ, 128).transpose(3, 0, 2, 1)
    cst[:, C_FCONV:C_FCONV + 264] = fc.reshape(128, 264)
    t = np.arange(16, dtype=f)
    for c in range(4):
        w = float(2 ** (c + 1))
        cnt = np.minimum(t + 1.0, w) if q == 0 else np.full(16, w, f)
        cst[:, C_INVC + c * 16:C_INVC + (c + 1) * 16] = (f(1.0) / cnt.astype(f))[None, :]
    bs = inp["sgu_b"].reshape(L, 4, 2, 128)
    bs = np.repeat(bs, 64, axis=2).transpose(2, 0, 1, 3)
    cst[:, C_BST:C_BST + 1024] = bs.reshape(128, 1024)
    cst[:, C_LNGP:C_LNGP + 8] = pk(inp["sgu_ln"]).reshape(128, 8)
    cst[:, C_EPS] = EPS
    return cst


def _constsb(inp):
    f = np.float32
    cb = np.zeros((128, NCSTB), f)
    wst = inp["sgu_w"].transpose(3, 0, 1, 2)
    cb[:, B_WST:B_WST + 2048] = wst.reshape(128, 2048)
    pw = inp["pool_w"].transpose(2, 0, 1, 3)
    cb[:, B_POOLW:B_POOLW + 1024] = pw.reshape(128, 1024)
    cb[:, B_ONES:B_ONES + 128] = 1.0 / 1024
    j = np.arange(128)[:, None] // 64
    i = np.arange(128)[None, :] // 64
    cb[:, B_MASK:B_MASK + 128] = (j <= i).astype(f)
    return cb


_NC_CACHE = {}


def _get_nc(layers, final_norm):
    key = (tuple(layers), final_norm)
    if key not in _NC_CACHE:
        _NC_CACHE[key] = build(list(layers), final_norm)
    return _NC_CACHE[key]


def _launch(x, inp, layers, final_norm):
    nc = _get_nc(layers, final_norm)
    cb = _constsb(inp)
    in_maps = []
    for core in range(NCORES):
        b, q = core // 4, core % 4
        s0 = q * OWN
        xs = np.zeros((TOK, D), np.float32)
        if q == 0:
            xs[HALO:] = x[b, 0:OWN]
        else:
            xs = x[b, s0 - HALO:s0 + OWN]
        in_maps.append({
            "xT": np.ascontiguousarray(xs.T),
            "w_in": inp["w_in"], "w_br_a": inp["w_br_a"], "w_br_b": inp["w_br_b"], "w_br_c": inp["w_br_c"],
            "w_o": inp["w_o"], "w_up": inp["w_up"], "w_down": inp["w_down"],
            "cst": _consts(inp, q), "cstb": cb,
        })
    res = run_bass_kernel_spmd(nc, in_maps, core_ids=list(range(NCORES)))
    out = np.empty((2, S, D), np.float32)
    for core in range(NCORES):
        b, q = core // 4, core % 4
        out[b, q * OWN:(q + 1) * OWN, :] = np.asarray(res.results[core]["outT"]).T
    return out


FUSED = True


def kernel(**inputs):
    inp = {k: np.ascontiguousarray(np.asarray(v, dtype=np.float32)) for k, v in inputs.items()}
    x = inp["x"]
    if FUSED:
        return _launch(x, inp, (0, 1), True)
    x = _launch(x, inp, (0,), False)
    return _launch(x, inp, (1,), True)
```
